# Optimizing a Trainium2 kernel written in Bass

```python
import math
import jax, jax.numpy as jnp
from jax import lax
import numpy as np

D_MODEL = 2048
BATCH = 1
SEQ = 8192
DEPTH = 4

CHUNK = 64
N_MIXERS = 4
N_CONV_LAYERS = len(range(0, DEPTH, N_MIXERS))
N_POOL_LAYERS = len(range(1, DEPTH, N_MIXERS))
N_ATT_LAYERS = len(range(2, DEPTH, N_MIXERS))
N_SSM_LAYERS = len(range(3, DEPTH, N_MIXERS))
D_FF = 4 * D_MODEL
CONV_WIDTH = 3
POOL_WINDOWS = (2, 4, 8, 16)
N_POOL_GROUPS = len(POOL_WINDOWS)
POOL_GROUP = D_MODEL // N_POOL_GROUPS
ATT_HEAD_DIM = 128
ATT_HEADS = D_MODEL // ATT_HEAD_DIM
ATT_LEFT_CHUNKS = 8
ATT_PAD = ATT_LEFT_CHUNKS * CHUNK
ATT_BAND = ATT_PAD + CHUNK
REL_CLIP = 256
MASK_VALUE = -1e30
SSM_GROUP = 16
SSM_GROUPS = D_MODEL // SSM_GROUP
SSM_STATE = 64
SSM_BLOCK = 16
SSM_N_BLOCKS = SSM_GROUPS // SSM_BLOCK
DT_MIN = 1e-3
DT_MAX = 1e-1
RMS_EPS = 1e-6

kernel_name = 'interleaved_hybrid_streaming_encoder'


def rms_norm(x, gain):
    xf = x.astype(jnp.float32)
    y = xf * lax.rsqrt(jnp.mean(xf * xf, axis=-1, keepdims=True) + RMS_EPS)
    return (y * gain.astype(jnp.float32)).astype(x.dtype)


def squared_relu_mlp(h, w1, w2):
    a = jax.nn.relu(h @ w1)
    return (a * a) @ w2


def short_conv_mixer(h, w_in, conv_w, w_out):
    b_gate, c_gate, v = jnp.split(h @ w_in, 3, axis=-1)
    u = c_gate * v
    conv = lax.conv_general_dilated(
        u, conv_w.reshape(CONV_WIDTH, 1, D_MODEL).astype(u.dtype),
        window_strides=(1,), padding=[(CONV_WIDTH - 1, 0)],
        dimension_numbers=('NWC', 'WIO', 'NWC'), feature_group_count=D_MODEL)
    return (b_gate * conv) @ w_out


def pool_mixer(h, w_in, w_group, scale):
    b, s, _ = h.shape
    u = (h @ w_in).astype(jnp.float32).reshape(b, s, N_POOL_GROUPS, POOL_GROUP)
    csum = jnp.cumsum(u, axis=1)
    pos = jnp.arange(1, s + 1, dtype=jnp.float32)
    outs = []
    for gi, w in enumerate(POOL_WINDOWS):
        c = csum[:, :, gi]
        lagged = jnp.pad(c, ((0, 0), (w, 0), (0, 0)))[:, :s]
        count = jnp.minimum(pos, float(w))[None, :, None]
        outs.append((c - lagged) / count - u[:, :, gi])
    pooled = jnp.stack(outs, axis=2).astype(h.dtype)
    y = jnp.einsum('bsgc,gcd->bsgd', pooled, w_group)
    return y.reshape(b, s, D_MODEL) * scale


def chunk_attention_mixer(h, w_qkv, q_gain, k_gain, rel_bias, w_out):
    b, s, _ = h.shape
    nc = s // CHUNK
    qkv = (h @ w_qkv).reshape(b, s, 3, ATT_HEADS, ATT_HEAD_DIM)
    q = rms_norm(qkv[:, :, 0], q_gain)
    k = rms_norm(qkv[:, :, 1], k_gain)
    v = qkv[:, :, 2]
    k_pad = jnp.pad(k, ((0, 0), (ATT_PAD, 0), (0, 0), (0, 0)))
    v_pad = jnp.pad(v, ((0, 0), (ATT_PAD, 0), (0, 0), (0, 0)))
    q_idx = jnp.arange(CHUNK)[:, None] + ATT_PAD
    k_idx = jnp.arange(ATT_BAND)[None, :]
    rel = jnp.clip(q_idx - k_idx, -REL_CLIP, REL_CLIP) + REL_CLIP
    bias = rel_bias[:, rel].astype(jnp.float32)
    q_chunks = q.reshape(b, nc, CHUNK, ATT_HEADS, ATT_HEAD_DIM).transpose(1, 0, 2, 3, 4)
    scale = ATT_HEAD_DIM ** -0.5

    def one_chunk(args):
        c, q_c = args
        start = c * CHUNK
        k_band = lax.dynamic_slice_in_dim(k_pad, start, ATT_BAND, axis=1)
        v_band = lax.dynamic_slice_in_dim(v_pad, start, ATT_BAND, axis=1)
        scores = jnp.einsum('bqhd,bkhd->bhqk', q_c, k_band).astype(jnp.float32) * scale + bias
        key_pos = start - ATT_PAD + jnp.arange(ATT_BAND)
        scores = jnp.where((key_pos >= 0)[None, None, None, :], scores, MASK_VALUE)
        probs = jax.nn.softmax(scores, axis=-1).astype(v_band.dtype)
        return jnp.einsum('bhqk,bkhd->bqhd', probs, v_band)

    out = lax.map(one_chunk, (jnp.arange(nc), q_chunks))
    out = out.transpose(1, 0, 2, 3, 4).reshape(b, s, D_MODEL)
    return out @ w_out


def _ssm_combine(left, right):
    a1, b1 = left
    a2, b2 = right
    return a1 * a2, a2 * b1 + b2


def s5_mixer(h, a_re, a_im, log_dt, b_re, b_im, c_re, c_im, d_skip, w_glu):
    b, s, _ = h.shape
    f32 = jnp.float32
    u_flat = h.astype(f32)
    lam = lax.complex(a_re.astype(f32), a_im.astype(f32))
    dt = jnp.exp(log_dt.astype(f32))[:, None]
    a_bar = jnp.exp(lam * dt)
    b_mat = lax.complex(b_re.astype(f32), b_im.astype(f32))
    b_bar = ((a_bar - 1.0) / lam)[..., None] * b_mat
    c_mat = lax.complex(c_re.astype(f32), c_im.astype(f32))

    def to_blocks(t):
        return t.reshape(SSM_N_BLOCKS, SSM_BLOCK, *t.shape[1:])

    u_blk = u_flat.reshape(b, s, SSM_N_BLOCKS, SSM_BLOCK, SSM_GROUP).transpose(2, 0, 1, 3, 4)

    def scan_block(args):
        u_b, a_b, bb_b, c_b = args
        bu = jnp.einsum('bsgc,gnc->bsgn', u_b.astype(jnp.complex64), bb_b)
        a_t = jnp.broadcast_to(a_b, bu.shape)
        _, states = lax.associative_scan(_ssm_combine, (a_t, bu), axis=1)
        return jnp.real(jnp.einsum('bsgn,gcn->bsgc', states, c_b))

    y = lax.map(scan_block, (u_blk, to_blocks(a_bar), to_blocks(b_bar), to_blocks(c_mat)))
    y = y.transpose(1, 2, 0, 3, 4).reshape(b, s, D_MODEL) + d_skip.astype(f32) * u_flat
    z = jax.nn.gelu(y).astype(h.dtype)
    val, gate = jnp.split(z @ w_glu, 2, axis=-1)
    return val * jax.nn.sigmoid(gate)


def setup_inputs(seed: int = 0) -> dict:
    key = jax.random.key(seed)
    ks = jax.random.split(key, 32)
    f32 = jnp.float32

    def nrm(k, shape, scale):
        return jax.random.normal(k, shape, f32) * scale

    nA, nB, nC, nD = N_CONV_LAYERS, N_POOL_LAYERS, N_ATT_LAYERS, N_SSM_LAYERS
    G, N = SSM_GROUPS, SSM_STATE
    inv_d = D_MODEL ** -0.5
    return {
        'x': nrm(ks[0], (BATCH, SEQ, D_MODEL), 1.0),
        'norm_mix': 1.0 + nrm(ks[1], (DEPTH, D_MODEL), 0.02),
        'norm_mlp': 1.0 + nrm(ks[2], (DEPTH, D_MODEL), 0.02),
        'mlp_w1': nrm(ks[3], (DEPTH, D_MODEL, D_FF), inv_d),
        'mlp_w2': nrm(ks[4], (DEPTH, D_FF, D_MODEL), D_FF ** -0.5),
        'conv_w_in': nrm(ks[5], (nA, D_MODEL, 3 * D_MODEL), inv_d),
        'conv_w': nrm(ks[6], (nA, CONV_WIDTH, D_MODEL), CONV_WIDTH ** -0.5),
        'conv_w_out': nrm(ks[7], (nA, D_MODEL, D_MODEL), inv_d),
        'pool_w_in': nrm(ks[8], (nB, D_MODEL, D_MODEL), inv_d),
        'pool_w_group': nrm(ks[9], (nB, N_POOL_GROUPS, POOL_GROUP, POOL_GROUP), POOL_GROUP ** -0.5),
        'pool_scale': 1.0 + nrm(ks[10], (nB, D_MODEL), 0.1),
        'att_w_qkv': nrm(ks[11], (nC, D_MODEL, 3 * D_MODEL), inv_d),
        'att_q_norm': 1.0 + nrm(ks[12], (nC, ATT_HEAD_DIM), 0.02),
        'att_k_norm': 1.0 + nrm(ks[13], (nC, ATT_HEAD_DIM), 0.02),
        'att_rel_bias': nrm(ks[14], (nC, ATT_HEADS, 2 * REL_CLIP + 1), 0.5),
        'att_w_out': nrm(ks[15], (nC, D_MODEL, D_MODEL), inv_d),
        'ssm_a_re': -0.5 + nrm(ks[16], (nD, G, N), 0.01),
        'ssm_a_im': math.pi * jnp.arange(N, dtype=f32) + nrm(ks[17], (nD, G, N), 0.01),
        'ssm_log_dt': jax.random.uniform(ks[18], (nD, G), f32, math.log(DT_MIN), math.log(DT_MAX)),
        'ssm_b_re': nrm(ks[19], (nD, G, N, SSM_GROUP), (2 * SSM_GROUP) ** -0.5),
        'ssm_b_im': nrm(ks[20], (nD, G, N, SSM_GROUP), (2 * SSM_GROUP) ** -0.5),
        'ssm_c_re': nrm(ks[21], (nD, G, SSM_GROUP, N), (2 * N) ** -0.5 * 4.0),
        'ssm_c_im': nrm(ks[22], (nD, G, SSM_GROUP, N), (2 * N) ** -0.5 * 4.0),
        'ssm_d': nrm(ks[23], (nD, D_MODEL), 1.0),
        'ssm_w_glu': nrm(ks[24], (nD, D_MODEL, 2 * D_MODEL), inv_d),
    }


def reference(x, norm_mix, norm_mlp, mlp_w1, mlp_w2, conv_w_in, conv_w, conv_w_out,
              pool_w_in, pool_w_group, pool_scale, att_w_qkv, att_q_norm, att_k_norm,
              att_rel_bias, att_w_out, ssm_a_re, ssm_a_im, ssm_log_dt, ssm_b_re, ssm_b_im,
              ssm_c_re, ssm_c_im, ssm_d, ssm_w_glu):
    for i in range(DEPTH):
        kind = i % N_MIXERS
        j = i // N_MIXERS
        h = rms_norm(x, norm_mix[i])
        if kind == 0:
            m = short_conv_mixer(h, conv_w_in[j], conv_w[j], conv_w_out[j])
        elif kind == 1:
            m = pool_mixer(h, pool_w_in[j], pool_w_group[j], pool_scale[j])
        elif kind == 2:
            m = chunk_attention_mixer(h, att_w_qkv[j], att_q_norm[j], att_k_norm[j],
                                      att_rel_bias[j], att_w_out[j])
        else:
            m = s5_mixer(h, ssm_a_re[j], ssm_a_im[j], ssm_log_dt[j], ssm_b_re[j], ssm_b_im[j],
                         ssm_c_re[j], ssm_c_im[j], ssm_d[j], ssm_w_glu[j])
        x = x + m.astype(x.dtype)
        h = rms_norm(x, norm_mlp[i])
        x = x + squared_relu_mlp(h, mlp_w1[i], mlp_w2[i]).astype(x.dtype)
    return x
```

```python
import contextlib
import math
import numpy as np
import concourse.bass as bass
import concourse.mybir as mybir
from concourse.bass_utils import run_bass_kernel_spmd

F32 = mybir.dt.float32
BF16 = mybir.dt.bfloat16
I32 = mybir.dt.int32
AF = mybir.ActivationFunctionType
ALU = mybir.AluOpType

D = 2048
SEQ = 8192
NCORE = 8
T = SEQ // NCORE
NCH = D // 128
DFF = 4 * D
RING = 8
RMS_EPS = 1e-6
HALO = {0: 2, 1: 16, 2: 512, 3: 0}
NSLAB = {0: 64, 1: 20, 2: 64, 3: 32}
NSLAB_MLP = 128

ENGS = ("pe", "act", "dve", "pool", "sp")


class Prog:
    def __init__(self, nc):
        self.nc = nc
        self.ops = {e: [] for e in ENGS}
        self.seen = {e: {} for e in ENGS}
        self.lastw = {}
        self.readers = {}
        self.dmacnt = {}
        self.needed = {e: set() for e in ENGS}
        self.lastE = {}
        self.collkeys = []

    def _filter(self, e, evs):
        best = {}
        for kind, key, val in evs:
            if kind == "E" and key == e:
                continue
            sk = (kind, key)
            if self.seen[e].get(sk, -1) >= val:
                continue
            if best.get(sk, -1) < val:
                best[sk] = val
        waits = []
        for sk, val in best.items():
            self.seen[e][sk] = val
            waits.append((sk[0], sk[1], val))
            if sk[0] == "E":
                self.needed[sk[1]].add(val)
        return waits

    def _deps(self, e, r, w):
        evs = []
        for res in r:
            if res in self.lastw:
                evs.append(self.lastw[res])
        for res in w:
            if res in self.lastw:
                evs.append(self.lastw[res])
            evs.extend(self.readers.get(res, ()))
        return self._filter(e, evs)

    def _commit(self, ev, r, w):
        for res in w:
            self.lastw[res] = ev
            self.readers[res] = []
        for res in r:
            self.readers.setdefault(res, []).append(ev)

    def op(self, e, fn, r=(), w=()):
        waits = self._deps(e, r, w)
        ev = ("E", e, len(self.ops[e]))
        self.ops[e].append(dict(waits=waits, fn=fn, ev=ev))
        self.lastE[e] = ev
        self._commit(ev, r, w)
        return ev

    def dma(self, e, fn, semkey, r=(), w=()):
        waits = self._deps(e, r, w)
        self.dmacnt[semkey] = self.dmacnt.get(semkey, 0) + 16
        ev = ("D", semkey, self.dmacnt[semkey])
        self.ops[e].append(dict(waits=waits, fn=fn, ev=ev))
        self._commit(ev, r, w)
        return ev

    def coll(self, e, fn, semkey, r=(), w=()):
        waits = self._deps(e, r, w)
        assert semkey not in self.collkeys
        self.collkeys.append(semkey)
        ev = ("C", semkey, 1)
        self.ops[e].append(dict(waits=waits, fn=fn, ev=ev))
        self._commit(ev, r, w)
        return ev

    def barrier(self):
        evs = list(self.lastE.values()) + [("D", k, v) for k, v in self.dmacnt.items()] + [("C", k, 1) for k in self.collkeys]
        for e in ENGS:
            waits = self._filter(e, evs)
            if waits:
                self.ops[e].append(dict(waits=waits, fn=None, ev=None))

    def emit(self):
        nc = self.nc
        stack = contextlib.ExitStack()
        sems = {}
        for e in ENGS:
            sems[("E", e)] = stack.enter_context(nc.semaphore("s_" + e))
        for k in self.dmacnt:
            sems[("D", k)] = stack.enter_context(nc.semaphore("d_" + str(k)))
        for k in self.collkeys:
            sems[("C", k)] = stack.enter_context(nc.semaphore("c_" + str(k)))
        cum = {}
        for e in ENGS:
            c = 0
            m = {}
            for i in range(len(self.ops[e])):
                if i in self.needed[e]:
                    c += 1
                m[i] = c
            cum[e] = m

        def run(e, eng):
            for i, o in enumerate(self.ops[e]):
                for kind, key, val in o["waits"]:
                    v = cum[key][val] if kind == "E" else val
                    eng.wait_ge(sems[(kind, key)], v)
                if o["fn"] is None:
                    continue
                ins = o["fn"](eng)
                ev = o["ev"]
                if ev[0] == "D":
                    ins.then_inc(sems[("D", ev[1])], 16)
                elif ev[0] == "C":
                    ins.then_inc(sems[("C", ev[1])])
                elif i in self.needed[e]:
                    ins.then_inc(sems[("E", e)], 1)

        with stack:
            with nc.Block() as block:
                @block.tensor
                def _(eng):
                    run("pe", eng)

                @block.scalar
                def _(eng):
                    run("act", eng)

                @block.vector
                def _(eng):
                    run("dve", eng)

                @block.gpsimd
                def _(eng):
                    run("pool", eng)

                @block.sync
                def _(eng):
                    run("sp", eng)


PC_NM = 0
PC_NF = 64
PC_CW = 128
PC_PS = 176
PC_QG = 192
PC_KG = 193
PC_SD = 194
PC_SG1 = 210
PC_GM = 211
NPAR = 224


class Slab:
    def __init__(self, idx, ap, res):
        self.idx = idx
        self.ap = ap
        self.res = res


class Builder:
    def __init__(self, layer, stage="full", fused_layers=None):
        self.fused = fused_layers is not None
        self.fused_layers = fused_layers
        self.layer = layer
        self.kind = layer % 4
        self.stage = stage
        self.H = HALO[self.kind]
        self.specs = []
        self.ns_total = NSLAB[self.kind] + (NSLAB_MLP if stage != "mixer" else 0)
        if stage == "ssm_a":
            self.ns_total = 0
        if self.fused:
            self.ns_total = sum(NSLAB[l % 4] + NSLAB_MLP for l in fused_layers)
        self.nc = bass.Bass("TRN2", target_bir_lowering=False)
        self.p = Prog(self.nc)
        self.st = contextlib.ExitStack()
        self.uid = 0
        self.ccnt = 0
        self.ccnt2 = 0

    def sb(self, shape, dtype, name=None, stack=None):
        self.uid += 1
        name = f"sb{self.uid}_" + (name or "t")
        return (stack or self.st).enter_context(self.nc.sbuf_tensor(name, list(shape), dtype))

    def dram(self, name, shape, dtype, kind):
        return self.nc.dram_tensor(name, list(shape), dtype, kind=kind).ap()

    def ring_init(self):
        if self.ns_total == 0:
            return
        self.ring = [self.sb([128, 16, 128], BF16, f"ring{i}") for i in range(RING)]
        if self.ns_total == 0:
            return
        self.wsl = self.dram("wsl", [self.ns_total, 128, 2048], F32, "ExternalInput")
        for i in range(min(RING, self.ns_total)):
            self._slab_dma(i)

    def _slab_dma(self, idx):
        slot = idx % RING
        dst = self.ring[slot]
        src = self.wsl[idx]
        self.p.dma("pool", lambda e: e.dma_start(out=dst[:].rearrange("p a c -> p (a c)"), in_=src),
                   f"w{slot}", w=[f"ring{slot}"])

    def slab(self, spec):
        idx = len(self.specs)
        assert idx < self.ns_total, (idx, spec)
        self.specs.append(spec)
        slot = idx % RING
        return Slab(idx, self.ring[slot], f"ring{slot}")

    def slab_done(self, s):
        nxt = s.idx + RING
        if nxt < self.ns_total:
            self._slab_dma(nxt)

    def setup(self):
        p = self.p
        nc = self.nc
        self.par_d = self.dram("par", [128, NPAR], F32, "ExternalInput")
        self.par = self.sb([128, NPAR], F32, "par")
        p.dma("sp", lambda e: e.dma_start(out=self.par[:], in_=self.par_d[:, :]), "par", w=["par"])
        self.onesm = self.sb([128, 128], BF16, "onesm")
        self.onesd = self.sb([128, 128], BF16, "onesd")
        self.ones1 = self.sb([128, 128], BF16, "ones1")
        p.op("dve", lambda e: e.memset(self.onesm[:], 1.0 / D), w=["onesm"])
        p.op("dve", lambda e: e.memset(self.onesd[:], 1.0 / 128), w=["onesd"])
        p.op("dve", lambda e: e.memset(self.ones1[:], 1.0), w=["ones1"])
        self.oneb = self.sb([128, 1], F32, "oneb")
        p.op("dve", lambda e: e.memset(self.oneb[:], 1.0), w=["oneb"])
        self.epsb = self.sb([128, 1], F32, "epsb")
        p.op("dve", lambda e: e.memset(self.epsb[:], RMS_EPS), w=["epsb"])
        self.PS = [self.st.enter_context(nc.psum_tensor(f"ps{i}", [128, 1024], F32)) for i in range(4)]
        self.sq = [self.sb([128, 1024], BF16, f"sq{i}") for i in range(2)]
        self.rstd = self.sb([128, 1024], F32, "rstd")

    def psname(self, i):
        return f"PS{i}"

    def norm(self, xsrc, n, gcol, dst, doff, xres, hres):
        p = self.p
        ps = self.PS[0]
        segs = [(s, min(512, n - s)) for s in range(0, n, 512)]
        for c in range(NCH):
            sq = self.sq[c % 2]
            p.op("act", lambda e, sq=sq, c=c: e.activation(out=sq[:, :n], in_=xsrc[:, c, :n], func=AF.Square),
                 r=[xres(c)], w=[f"sq{c % 2}"])
            for (s, m) in segs:
                p.op("pe", lambda e, sq=sq, s=s, m=m, c=c: e.matmul(ps[:, s:s + m], lhsT=self.onesm[:], rhs=sq[:, s:s + m],
                                                                      start=(c == 0), stop=(c == NCH - 1)),
                     r=[f"sq{c % 2}", "onesm"], w=["PS0"])
        p.op("act", lambda e: e.activation(out=self.rstd[:, :n], in_=ps[:, :n], func=AF.Sqrt, bias=self.epsb[:], scale=1.0),
             r=["PS0", "epsb"], w=["rstd"])
        p.op("dve", lambda e: e.reciprocal(out=self.rstd[:, :n], in_=self.rstd[:, :n]), r=["rstd"], w=["rstd"])
        for c in range(NCH):
            p.op("dve", lambda e, c=c: e.scalar_tensor_tensor(out=dst[:, c, doff:doff + n], in0=xsrc[:, c, :n],
                                                               scalar=self.par[:, gcol + c:gcol + c + 1], op0=ALU.mult,
                                                               in1=self.rstd[:, :n], op1=ALU.mult),
                 r=[xres(c), "rstd", "par"], w=[hres(c)])

    def mlp(self):
        p = self.p
        l = self.layer
        xT, hT, HO = self.xT, self.hT, self.HO
        st = contextlib.ExitStack()
        with st:
            aT = [self.sb([128, 4, 1024], BF16, f"aT{i}", st) for i in range(2)]
            rl = [self.sb([128, 1024], F32, f"rl{i}", st) for i in range(2)]
            self.norm(xT, T, PC_NF + 16 * l, hT, HO, lambda c: f"x{c}", lambda c: f"h{c}")

            def w1_phase(g):
                for jj in range(4):
                    j = 4 * g + jj
                    s = self.slab(("w1", l, j))
                    ps = self.PS[j % 2]
                    for k in range(NCH):
                        for hf in range(2):
                            p.op("pe", lambda e, s=s, ps=ps, k=k, hf=hf: e.matmul(
                                ps[:, hf * 512:(hf + 1) * 512], lhsT=s.ap[:, k, :],
                                rhs=hT[:, k, HO + hf * 512:HO + (hf + 1) * 512], start=(k == 0), stop=(k == NCH - 1)),
                                r=[s.res, f"h{k}"], w=[f"PS{j % 2}"])
                    self.slab_done(s)
                    p.op("act", lambda e, ps=ps, j=j: e.activation(out=rl[j % 2][:], in_=ps[:], func=AF.Relu),
                         r=[f"PS{j % 2}"], w=[f"rl{j % 2}"])
                    p.op("act", lambda e, j=j, g=g, jj=jj: e.activation(out=aT[g % 2][:, jj, :], in_=rl[j % 2][:], func=AF.Square),
                         r=[f"rl{j % 2}"], w=[f"aT{g % 2}"])

            def w2_phase(g):
                ss = [self.slab(("w2", l, 4 * g + jj)) for jj in range(4)]
                for f in range(NCH):
                    for hf in range(2):
                        bi = (f * 2 + hf) % 4
                        ps = self.PS[2 + bi // 2]
                        po = (bi % 2) * 512
                        rn = f"PSO{bi}"
                        for jj in range(4):
                            p.op("pe", lambda e, s=ss[jj], ps=ps, po=po, f=f, jj=jj, hf=hf, g=g: e.matmul(
                                ps[:, po:po + 512], lhsT=s.ap[:, f, :], rhs=aT[g % 2][:, jj, hf * 512:(hf + 1) * 512],
                                start=(jj == 0), stop=(jj == 3)),
                                r=[ss[jj].res, f"aT{g % 2}"], w=[rn])
                        p.op("dve", lambda e, ps=ps, po=po, f=f, hf=hf: e.tensor_tensor(
                            out=xT[:, f, hf * 512:(hf + 1) * 512], in0=xT[:, f, hf * 512:(hf + 1) * 512],
                            in1=ps[:, po:po + 512], op=ALU.add),
                            r=[rn, f"x{f}"], w=[f"x{f}"])
                for s in ss:
                    self.slab_done(s)

            NG = DFF // 128 // 4
            w1_phase(0)
            for g in range(1, NG):
                w1_phase(g)
                w2_phase(g - 1)
            w2_phase(NG - 1)
            p.barrier()

    def load_x(self):
        self.xin = self.dram("xin", [D, T], F32, "ExternalInput")
        xv = self.xin.rearrange("(c p) t -> p c t", p=128)
        for q in range(4):
            self.p.dma("sp", lambda e, q=q: e.dma_start(out=self.xT[:, 4 * q:4 * q + 4, :], in_=xv[:, 4 * q:4 * q + 4, :]),
                       f"xin{q}", w=[f"x{c}" for c in range(4 * q, 4 * q + 4)])

    def store_x(self):
        self.xout = self.dram("xout", [D, T], F32, "ExternalOutput")
        xv = self.xout.rearrange("(c p) t -> p c t", p=128)
        for q in range(4):
            self.p.dma("sp", lambda e, q=q: e.dma_start(out=xv[:, 4 * q:4 * q + 4, :], in_=self.xT[:, 4 * q:4 * q + 4, :]),
                       "xout", r=[f"x{c}" for c in range(4 * q, 4 * q + 4)])
        self.p.barrier()

    def halo_exchange(self):
        p = self.p
        H, HO, hT, l = self.H, self.HO, self.hT, self.layer
        if H == 0:
            return
        hres = [f"h{c}" for c in range(NCH)]
        NE = NCH * H
        W = NE // 2
        st = contextlib.ExitStack()
        with st:
            tl = self.sb([128, NE], BF16, "hx_tl", st)
            hl = self.sb([128, NE], BF16, "hx_hl", st)
            nchunk = (W + 63) // 64
            stg = self.sb([128, nchunk, NCORE, min(64, W)], F32, "hx_stg", st) if nchunk <= 2 else None
            stgs = [self.sb([128, NCORE, 64], F32, f"hx_stg{i}", st) for i in range(4)] if stg is None else None
            p.op("dve", lambda e: e.tensor_copy(out=tl[:].rearrange("p (c t) -> p c t", c=NCH), in_=hT[:, :, HO + T - H:HO + T]),
                 r=hres, w=[f"cgsrc_hx{l}"])
            tl32 = tl[:].bitcast(F32)

            def dstf(k, w0, wn):
                if stg is not None:
                    return stg[:, k, :, 0:wn]
                return stgs[k % 4][:, :, 0:wn]

            CW = 64
            chunks = [(k, w0, min(CW, W - w0)) for k, w0 in enumerate(range(0, W, CW))]
            couts = [self.ag_send(tl32, w0, wn, f"hx{l}", k, f"cgsrc_hx{l}") for (k, w0, wn) in chunks]

            def recv(k, w0, wn):
                sres = f"hxstg{k % 4}"
                self.ag_recv(couts[k], dstf(k, w0, wn), f"hx{l}", k, sres)
                src16 = (stg[:, k, :, :] if stg is not None else stgs[k % 4][:, :, :])
                for r in range(NCORE - 1):
                    blk = src16[:, r, 0:wn].bitcast(BF16)
                    o = hl[:, 2 * w0:2 * w0 + 2 * wn]
                    if r == 0:
                        p.op("dve", lambda e, blk=blk, o=o: e.tensor_scalar(out=o, in0=blk, scalar1=self.cpar[:, 0:1], scalar2=None, op0=ALU.mult),
                             r=[sres, "cpar"], w=[f"hxhl{k}"])
                    else:
                        p.op("dve", lambda e, blk=blk, o=o, r=r: e.scalar_tensor_tensor(out=o, in0=blk, scalar=self.cpar[:, r:r + 1], op0=ALU.mult,
                                                                                      in1=o, op1=ALU.add),
                             r=[sres, "cpar", f"hxhl{k}"], w=[f"hxhl{k}"])
            for (k, w0, wn) in chunks:
                recv(k, w0, wn)
            k = len(chunks)
            p.op("dve", lambda e: e.tensor_copy(out=hT[:, :, HO - H:HO], in_=hl[:].rearrange("p (c t) -> p c t", c=NCH)),
                 r=[f"hxhl{i}" for i in range(k)], w=hres)
            p.barrier()

    def ag_send(self, src32, w0, wn, tag, k, sres):
        p = self.p
        nc = self.nc
        cin = nc.dram_tensor(f"cg_in_{tag}_{k}", [128, wn], F32)
        cout = nc.dram_tensor(f"cg_out_{tag}_{k}", [NCORE * 128, wn], F32)
        kk = self.ccnt % 4
        self.ccnt += 1
        p.dma("sp", lambda e: e.dma_start(out=cin.ap()[:, :], in_=src32[:, w0:w0 + wn]),
              f"cgi{kk}", r=[sres], w=[f"cgin_{tag}_{k}", f"cgikey{kk}"])
        p.coll("pool", lambda e: e.collective_compute(
            "AllGather", ALU.bypass, replica_groups=[list(range(NCORE))],
            ins=[cin.ap().opt()], outs=[cout.ap().opt()]), f"cc_{tag}_{k}",
            r=[f"cgin_{tag}_{k}"], w=[f"cgout_{tag}_{k}"])
        return cout

    def ag_recv(self, cout, dst, tag, k, dres):
        kk = self.ccnt2 % 4
        self.ccnt2 += 1
        self.p.dma("sp", lambda e: e.dma_start(out=dst, in_=cout.ap().rearrange("(r p) w -> p r w", p=128)),
                   f"cgo{kk}", r=[f"cgout_{tag}_{k}"], w=[dres, f"cgokey{kk}"])

    def allgather_words_one(self, src32, w0, wn, dst, tag, k, dres, sres=None):
        cout = self.ag_send(src32, w0, wn, tag, k, sres or f"cgsrc_{tag}")
        self.ag_recv(cout, dst, tag, k, dres)

    def get_halo(self):
        if self.fused:
            self.halo_exchange()
        else:
            self.load_halo_h()

    def load_halo_h(self):
        H = self.H
        if H == 0:
            return
        self.xh_d = self.dram("xh", [D, H], F32, "ExternalInput")
        xv = self.xh_d.rearrange("(c p) t -> p c t", p=128)
        blk = min(H, 256)
        st = contextlib.ExitStack()
        with st:
            xh = self.sb([128, NCH, blk], F32, "xhst", st)
            for b0 in range(0, H, blk):
                self.p.dma("sp", lambda e, b0=b0: e.dma_start(out=xh[:], in_=xv[:, :, b0:b0 + blk]), "xh", w=["xhst"])
                self.norm(xh, blk, PC_NM + 16 * self.layer, self.hT, self.HO - H + b0,
                          lambda c: "xhst", lambda c: f"h{c}")
            self.p.barrier()

    def mixer_conv(self):
        p = self.p
        l = self.layer
        xT, hT, HO = self.xT, self.hT, self.HO
        PS = self.PS
        self.norm(xT, T, PC_NM + 16 * l, hT, HO, lambda c: f"x{c}", lambda c: f"h{c}")
        self.get_halo()
        st = contextlib.ExitStack()
        with st:
            gT = self.sb([128, NCH, T], BF16, "gT", st)
            vsb = [self.sb([128, T + 2], F32, f"vsb{i}", st) for i in range(2)]
            ub = [self.sb([128, T + 2], F32, f"ub{i}", st) for i in range(2)]
            acc = [self.sb([128, T], F32, f"acc{i}", st) for i in range(2)]
            for f in range(NCH):
                b2 = f % 2
                sv = self.slab(("conv_in", 2, f))
                sc = self.slab(("conv_in", 1, f))
                sbb = self.slab(("conv_in", 0, f))
                for (s, pi) in ((sv, 0), (sc, 1)):
                    for k in range(NCH):
                        for hf in range(2):
                            p.op("pe", lambda e, s=s, pi=pi, k=k, hf=hf: e.matmul(
                                PS[pi][:, hf * 512:(hf + 1) * 512], lhsT=s.ap[:, k, :],
                                rhs=hT[:, k, HO + hf * 512:HO + (hf + 1) * 512], start=(k == 0), stop=(k == NCH - 1)),
                                r=[s.res, f"h{k}"], w=[f"PS{pi}"])
                for (s, po) in ((sc, 0), (sv, 512)):
                    for k in range(NCH):
                        p.op("pe", lambda e, s=s, po=po, k=k: e.matmul(
                            PS[3][:, po:po + 2], lhsT=s.ap[:, k, :], rhs=hT[:, k, HO - 2:HO],
                            start=(k == 0), stop=(k == NCH - 1)),
                            r=[s.res, f"h{k}"], w=["PS3"])
                for k in range(NCH):
                    for hf in range(2):
                        p.op("pe", lambda e, s=sbb, k=k, hf=hf: e.matmul(
                            PS[2][:, hf * 512:(hf + 1) * 512], lhsT=s.ap[:, k, :],
                            rhs=hT[:, k, HO + hf * 512:HO + (hf + 1) * 512], start=(k == 0), stop=(k == NCH - 1)),
                            r=[s.res, f"h{k}"], w=["PS2"])
                for s in (sv, sc, sbb):
                    self.slab_done(s)
                V, U, A = vsb[b2], ub[b2], acc[b2]
                p.op("act", lambda e, V=V: e.activation(out=V[:, 2:], in_=PS[0][:], func=AF.Copy), r=["PS0"], w=[f"vsb{b2}"])
                p.op("act", lambda e, V=V: e.activation(out=V[:, 0:2], in_=PS[3][:, 512:514], func=AF.Copy), r=["PS3"], w=[f"vsb{b2}"])
                p.op("dve", lambda e, V=V, U=U: e.tensor_tensor(out=U[:, 2:], in0=PS[1][:], in1=V[:, 2:], op=ALU.mult),
                     r=["PS1", f"vsb{b2}"], w=[f"ub{b2}"])
                p.op("dve", lambda e, V=V, U=U: e.tensor_tensor(out=U[:, 0:2], in0=PS[3][:, 0:2], in1=V[:, 0:2], op=ALU.mult),
                     r=["PS3", f"vsb{b2}"], w=[f"ub{b2}"])
                cw = lambda k, f=f: self.par[:, PC_CW + 16 * k + f:PC_CW + 16 * k + f + 1]
                p.op("dve", lambda e, U=U, A=A, cw=cw: e.tensor_scalar(out=A[:], in0=U[:, 2:T + 2], scalar1=cw(2), scalar2=None, op0=ALU.mult),
                     r=[f"ub{b2}", "par"], w=[f"acc{b2}"])
                p.op("dve", lambda e, U=U, A=A, cw=cw: e.scalar_tensor_tensor(out=A[:], in0=U[:, 1:T + 1], scalar=cw(1), op0=ALU.mult,
                                                                              in1=A[:], op1=ALU.add),
                     r=[f"ub{b2}", "par", f"acc{b2}"], w=[f"acc{b2}"])
                p.op("dve", lambda e, U=U, A=A, cw=cw: e.scalar_tensor_tensor(out=A[:], in0=U[:, 0:T], scalar=cw(0), op0=ALU.mult,
                                                                              in1=A[:], op1=ALU.add),
                     r=[f"ub{b2}", "par", f"acc{b2}"], w=[f"acc{b2}"])
                p.op("dve", lambda e, A=A, f=f: e.tensor_tensor(out=gT[:, f, :], in0=PS[2][:], in1=A[:], op=ALU.mult),
                     r=["PS2", f"acc{b2}"], w=[f"g{f}"])
            self.out_proj(gT, lambda k: f"g{k}", "conv_out")
            p.barrier()

    def out_proj(self, srcT, sres, specname, scale_col=None):
        p = self.p
        PS, xT = self.PS, self.xT
        for f in range(NCH):
            s = self.slab((specname, f))
            pi = f % 4
            for k in range(NCH):
                for hf in range(2):
                    p.op("pe", lambda e, s=s, pi=pi, k=k, hf=hf: e.matmul(
                        PS[pi][:, hf * 512:(hf + 1) * 512], lhsT=s.ap[:, k, :], rhs=srcT[:, k, hf * 512:(hf + 1) * 512],
                        start=(k == 0), stop=(k == NCH - 1)),
                        r=[s.res, sres(k)], w=[f"PS{pi}"])
            self.slab_done(s)
            p.op("dve", lambda e, pi=pi, f=f: e.tensor_tensor(out=xT[:, f, :], in0=xT[:, f, :], in1=PS[pi][:], op=ALU.add),
                 r=[f"PS{pi}", f"x{f}"], w=[f"x{f}"])


    def mixer_pool(self):
        p = self.p
        l = self.layer
        xT, hT, HO = self.xT, self.hT, self.HO
        PS = self.PS
        self.norm(xT, T, PC_NM + 16 * l, hT, HO, lambda c: f"x{c}", lambda c: f"h{c}")
        self.get_halo()
        st = contextlib.ExitStack()
        with st:
            invc_d = self.dram("invc", [128, 4 * T], F32, "ExternalInput")
            invc = self.sb([128, 4, T], F32, "invc", st)
            p.dma("sp", lambda e: e.dma_start(out=invc[:].rearrange("p a t -> p (a t)"), in_=invc_d[:, :]), "invc", w=["invc"])
            pT = self.sb([128, NCH, T], BF16, "pT", st)
            ub = [self.sb([128, T + 16], F32, f"pub{i}", st) for i in range(2)]
            sa = [self.sb([128, T + 16], F32, f"psa{i}", st) for i in range(2)]
            tmp = self.sb([128, T], F32, "ptmp", st)
            for f in range(NCH):
                b2 = f % 2
                s = self.slab(("pool_in", f))
                for k in range(NCH):
                    for hf in range(2):
                        p.op("pe", lambda e, s=s, b2=b2, k=k, hf=hf: e.matmul(
                            PS[b2][:, hf * 512:(hf + 1) * 512], lhsT=s.ap[:, k, :],
                            rhs=hT[:, k, HO + hf * 512:HO + (hf + 1) * 512], start=(k == 0), stop=(k == NCH - 1)),
                            r=[s.res, f"h{k}"], w=[f"PS{b2}"])
                for k in range(NCH):
                    p.op("pe", lambda e, s=s, b2=b2, k=k: e.matmul(
                        PS[2 + b2][:, 0:16], lhsT=s.ap[:, k, :], rhs=hT[:, k, 0:16], start=(k == 0), stop=(k == NCH - 1)),
                        r=[s.res, f"h{k}"], w=[f"PS{2 + b2}"])
                self.slab_done(s)
                U = ub[b2]
                p.op("act", lambda e, U=U, b2=b2: e.activation(out=U[:, 16:], in_=PS[b2][:], func=AF.Copy), r=[f"PS{b2}"], w=[f"pub{b2}"])
                p.op("act", lambda e, U=U, b2=b2: e.activation(out=U[:, 0:16], in_=PS[2 + b2][:, 0:16], func=AF.Copy), r=[f"PS{2 + b2}"], w=[f"pub{b2}"])
                gi = f // 4
                cur, curres = U, f"pub{b2}"
                sh = 1
                for stp in range(gi + 1):
                    nxt, nres = sa[stp % 2], f"psa{stp % 2}"
                    p.op("dve", lambda e, cur=cur, nxt=nxt, sh=sh: e.tensor_tensor(
                        out=nxt[:, sh:], in0=cur[:, sh:], in1=cur[:, 0:T + 16 - sh], op=ALU.add),
                        r=[curres], w=[nres])
                    cur, curres = nxt, nres
                    sh *= 2
                p.op("dve", lambda e, cur=cur, gi=gi: e.tensor_tensor(out=tmp[:], in0=cur[:, 16:], in1=invc[:, gi, :], op=ALU.mult),
                     r=[curres, "invc"], w=["ptmp"])
                p.op("dve", lambda e, U=U, f=f: e.tensor_tensor(out=pT[:, f, :], in0=tmp[:], in1=U[:, 16:], op=ALU.subtract),
                     r=["ptmp", f"pub{b2}"], w=[f"pt{f}"])
            for g in range(4):
                s = self.slab(("pool_g", g))
                for dch in range(4):
                    fo = 4 * g + dch
                    pi = fo % 4
                    for hf in range(2):
                        for cc in range(4):
                            p.op("pe", lambda e, s=s, pi=pi, dch=dch, cc=cc, hf=hf, g=g: e.matmul(
                                PS[pi][:, hf * 512:(hf + 1) * 512], lhsT=s.ap[:, dch * 4 + cc, :],
                                rhs=pT[:, 4 * g + cc, hf * 512:(hf + 1) * 512], start=(cc == 0), stop=(cc == 3)),
                                r=[s.res, f"pt{4 * g + cc}"], w=[f"PS{pi}"])
                    p.op("dve", lambda e, pi=pi, fo=fo: e.scalar_tensor_tensor(
                        out=xT[:, fo, :], in0=PS[pi][:], scalar=self.par[:, PC_PS + fo:PC_PS + fo + 1], op0=ALU.mult,
                        in1=xT[:, fo, :], op1=ALU.add),
                        r=[f"PS{pi}", f"x{fo}", "par"], w=[f"x{fo}"])
                self.slab_done(s)
            p.barrier()


    def bank(self):
        i = self.bank_i
        self.bank_i = (i + 1) % 8
        return self.PS[i // 2][:, (i % 2) * 512:(i % 2) * 512 + 512], f"B{i}"

    def mixer_att(self):
        p = self.p
        l = self.layer
        xT, hT, HO = self.xT, self.hT, self.HO
        self.bank_i = 0
        self.norm(xT, T, PC_NM + 16 * l, hT, HO, lambda c: f"x{c}", lambda c: f"h{c}")
        self.get_halo()
        p.barrier()
        SCALE = 128.0 ** -0.5
        st = contextlib.ExitStack()
        with st:
            bias_d = self.dram("biasT", [16, 128, 640], F32, "ExternalInput")
            onesh = self.sb([128, 128], BF16, "onesh", st)
            if self.fused:
                p.op("dve", lambda e: e.tensor_scalar(out=onesh[:], in0=self.ones1[:], scalar1=self.cpar[:, 16:17], scalar2=None, op0=ALU.mult),
                     r=["ones1", "cpar"], w=["onesh"])
            else:
                hm_d = self.dram("hmask", [128, 128], F32, "ExternalInput")
                hm32 = self.sb([128, 128], F32, "hm32", st)
                p.dma("sp", lambda e: e.dma_start(out=hm32[:], in_=hm_d[:, :]), "hm", w=["hm32"])
                p.op("dve", lambda e: e.tensor_copy(out=onesh[:], in_=hm32[:]), r=["hm32"], w=["onesh"])
            attnT = self.sb([128, 8, T], BF16, "attnT", st)
            sqb = [self.sb([128, 512], BF16, f"sqb{i}", st) for i in range(2)]
            rs = [self.sb([128, 512], F32, f"rs{i}", st) for i in range(2)]
            QnT = self.sb([128, T], BF16, "QnT", st)
            knT = self.sb([128, T + 512], BF16, "knT", st)
            Vsb = self.sb([128, 12, 128], BF16, "Vsb", st)
            stt = [self.sb([128, 640], F32, f"stt{i}", st) for i in range(2)]
            pTt = [self.sb([128, 640], BF16, f"pTt{i}", st) for i in range(2)]
            bia = [self.sb([128, 640], F32, f"bia{i}", st) for i in range(2)]
            rinv = [self.sb([128, 128], F32, f"rinv{i}", st) for i in range(2)]
            nrm_i = 0
            for h in range(16):
                hb = h % 2
                p.dma("sp", lambda e, h=h, hb=hb: e.dma_start(out=bia[hb][:], in_=bias_d[h]), f"bia{hb}", w=[f"bia{hb}"])
                s_q = self.slab(("att_qkv", 0, h))
                s_k = self.slab(("att_qkv", 1, h))
                s_v = self.slab(("att_qkv", 2, h))
                for (s, nseg, tok0, gcol, dst, dres) in ((s_q, 2, HO, PC_QG, QnT, "QnT"), (s_k, 3, 0, PC_KG, knT, "knT")):
                    for seg in range(nseg):
                        bq, bqn = self.bank()
                        for k in range(NCH):
                            p.op("pe", lambda e, s=s, bq=bq, k=k, t0=tok0 + seg * 512: e.matmul(
                                bq, lhsT=s.ap[:, k, :], rhs=hT[:, k, t0:t0 + 512], start=(k == 0), stop=(k == NCH - 1)),
                                r=[s.res, f"h{k}"], w=[bqn])
                        n2 = nrm_i % 2
                        nrm_i += 1
                        p.op("act", lambda e, bq=bq, n2=n2: e.activation(out=sqb[n2][:], in_=bq, func=AF.Square), r=[bqn], w=[f"sqb{n2}"])
                        bs, bsn = self.bank()
                        p.op("pe", lambda e, bs=bs, n2=n2: e.matmul(bs, lhsT=self.onesd[:], rhs=sqb[n2][:], start=True, stop=True),
                             r=[f"sqb{n2}", "onesd"], w=[bsn])
                        p.op("act", lambda e, bs=bs, n2=n2: e.activation(out=rs[n2][:], in_=bs, func=AF.Sqrt, bias=self.epsb[:], scale=1.0),
                             r=[bsn, "epsb"], w=[f"rs{n2}"])
                        p.op("dve", lambda e, n2=n2: e.reciprocal(out=rs[n2][:], in_=rs[n2][:]), r=[f"rs{n2}"], w=[f"rs{n2}"])
                        p.op("dve", lambda e, bq=bq, n2=n2, dst=dst, seg=seg, gcol=gcol: e.scalar_tensor_tensor(
                            out=dst[:, seg * 512:(seg + 1) * 512], in0=bq, scalar=self.par[:, gcol:gcol + 1], op0=ALU.mult,
                            in1=rs[n2][:], op1=ALU.mult),
                            r=[bqn, f"rs{n2}", "par"], w=[dres])
                for grp in range(3):
                    bv, bvn = self.bank()
                    for tt in range(4):
                        t0 = (4 * grp + tt) * 128
                        for k in range(NCH):
                            p.op("pe", lambda e, bv=bv, tt=tt, t0=t0, k=k, s=s_v: e.matmul(
                                bv[:, tt * 128:(tt + 1) * 128], lhsT=hT[:, k, t0:t0 + 128], rhs=s.ap[:, k, :],
                                start=(k == 0), stop=(k == NCH - 1)),
                                r=[s_v.res, f"h{k}"], w=[bvn])
                    p.op("act", lambda e, bv=bv, grp=grp: e.activation(
                        out=Vsb[:, 4 * grp:4 * grp + 4, :].rearrange("p a d -> p (a d)"), in_=bv, func=AF.Copy),
                        r=[bvn], w=["Vsb"])
                for s in (s_q, s_k, s_v):
                    self.slab_done(s)
                for i in range(8):
                    i2 = i % 2
                    b1, b1n = self.bank()
                    b2, b2n = self.bank()
                    for j in range(5):
                        dstp = b1[:, j * 128:(j + 1) * 128] if j < 4 else b2[:, 0:128]
                        p.op("pe", lambda e, dstp=dstp, i=i, j=j: e.matmul(
                            dstp, lhsT=knT[:, (i + j) * 128:(i + j + 1) * 128], rhs=QnT[:, i * 128:(i + 1) * 128],
                            start=True, stop=True),
                            r=["knT", "QnT"], w=[b1n if j < 4 else b2n])
                    p.op("dve", lambda e, b1=b1, i2=i2, hb=hb: e.scalar_tensor_tensor(
                        out=stt[i2][:, 0:512], in0=b1, scalar=SCALE, op0=ALU.mult, in1=bia[hb][:, 0:512], op1=ALU.add),
                        r=[b1n, f"bia{hb}"], w=[f"stt{i2}"])
                    p.op("dve", lambda e, b2=b2, i2=i2, hb=hb: e.scalar_tensor_tensor(
                        out=stt[i2][:, 512:640], in0=b2[:, 0:128], scalar=SCALE, op0=ALU.mult, in1=bia[hb][:, 512:640], op1=ALU.add),
                        r=[b2n, f"bia{hb}"], w=[f"stt{i2}"])
                    p.op("act", lambda e, i2=i2: e.activation(out=pTt[i2][:], in_=stt[i2][:], func=AF.Exp),
                         r=[f"stt{i2}"], w=[f"pTt{i2}"])
                    bo, bon = self.bank()
                    for j in range(5):
                        p.op("pe", lambda e, bo=bo, i=i, j=j, i2=i2: e.matmul(
                            bo[:, 0:128], lhsT=Vsb[:, i + j, :], rhs=pTt[i2][:, j * 128:(j + 1) * 128],
                            start=(j == 0), stop=(j == 4)),
                            r=["Vsb", f"pTt{i2}"], w=[bon])
                    for j in range(5):
                        on = onesh if (i + j) < 4 else self.ones1
                        p.op("pe", lambda e, bo=bo, j=j, i2=i2, on=on: e.matmul(
                            bo[:, 128:256], lhsT=on[:], rhs=pTt[i2][:, j * 128:(j + 1) * 128],
                            start=(j == 0), stop=(j == 4)),
                            r=["onesh", "ones1", f"pTt{i2}"], w=[bon])
                    p.op("dve", lambda e, bo=bo, i2=i2: e.reciprocal(out=rinv[i2][:], in_=bo[:, 128:256]), r=[bon], w=[f"rinv{i2}"])
                    p.op("dve", lambda e, bo=bo, i2=i2, h=h, i=i: e.tensor_tensor(
                        out=attnT[:, h % 8, i * 128:(i + 1) * 128], in0=bo[:, 0:128], in1=rinv[i2][:], op=ALU.mult),
                        r=[bon, f"rinv{i2}"], w=["attnT"])
                if h % 8 == 7:
                    rd = h // 8
                    for F in range(8):
                        s = self.slab(("att_o", rd, F))
                        for fi in range(2):
                            f = 2 * F + fi
                            for hf in range(2):
                                b, bn = self.bank()
                                for hh in range(8):
                                    p.op("pe", lambda e, s=s, b=b, fi=fi, hh=hh, hf=hf: e.matmul(
                                        b, lhsT=s.ap[:, fi * 8 + hh, :], rhs=attnT[:, hh, hf * 512:(hf + 1) * 512],
                                        start=(hh == 0), stop=(hh == 7)),
                                        r=[s.res, "attnT"], w=[bn])
                                p.op("dve", lambda e, b=b, f=f, hf=hf: e.tensor_tensor(
                                    out=xT[:, f, hf * 512:(hf + 1) * 512], in0=xT[:, f, hf * 512:(hf + 1) * 512], in1=b, op=ALU.add),
                                    r=[bn, f"x{f}"], w=[f"x{f}"])
                        self.slab_done(s)
            p.barrier()


    def sin_turns(self, dst, src, shape_tmp, tag):
        self.p.op("act", lambda e: e.activation(out=dst, in_=src, func=AF.Sin, scale=2.0 * math.pi), r=[tag], w=[tag])

    def mixer_ssm(self, mode):
        p = self.p
        l = self.layer
        xT, hT, HO = self.xT, self.hT, self.HO
        PS = self.PS
        TWO_PI = 2.0 * math.pi
        self.norm(xT, T, PC_NM + 16 * l, hT, HO, lambda c: f"x{c}", lambda c: f"h{c}")
        p.barrier()
        st = contextlib.ExitStack()
        with st:
            sA_d = self.dram("sA", [128, 3 * 128], F32, "ExternalInput")
            sC_d = self.dram("sC", [128, 2 * 128 * 16], F32, "ExternalInput")
            sB_d = self.dram("sB", [128, 5 * 1024], F32, "ExternalInput")
            sA = self.sb([128, 3, 128], F32, "sA", st)
            thf = self.sb([128, 128], F32, "thf", st)
            rdec = self.sb([128, 128], F32, "rdec", st)
            lnr = self.sb([128, 128], F32, "lnr", st)
            ki = self.sb([128, 128], I32, "ki128", st)
            bb1 = self.sb([128, 16, 128], BF16, "bb1", st)
            bb2 = self.sb([128, 16, 128], BF16, "bb2", st)
            L1 = self.sb([128, 128, 16], BF16, "L1", st)
            L2 = self.sb([128, 128, 16], BF16, "L2", st)
            Lz1 = [self.sb([128, 128], BF16, f"Lz1_{i}", st) for i in range(8)]
            Lz2 = [self.sb([128, 128], BF16, f"Lz2_{i}", st) for i in range(8)]
            Bg = [self.sb([128, 128], BF16, f"Bg{i}", st) for i in range(4)]
            gd = self.sb([128, 16], F32, "gd", st)
            Sin0 = self.sb([128, 128], F32, "Sin0", st)
            ZE = self.sb([128, 128], F32, "ZE", st)
            p.dma("sp", lambda e: e.dma_start(out=sA[:].rearrange("p a g -> p (a g)"), in_=sA_d[:, :]), "sA", w=["sA"])
            dtA = self.sb([128, 128], F32, "dtA", st)
            p.op("act", lambda e: e.activation(out=dtA[:], in_=sA[:, 2, :], func=AF.Exp), r=["sA"], w=["dtA"])
            p.op("dve", lambda e: e.tensor_tensor(out=lnr[:], in0=sA[:, 0, :], in1=dtA[:], op=ALU.mult), r=["sA", "dtA"], w=["lnr"])
            p.op("dve", lambda e: e.scalar_tensor_tensor(out=thf[:], in0=sA[:, 1, :], scalar=1.0 / TWO_PI, op0=ALU.mult,
                                                          in1=dtA[:], op1=ALU.mult), r=["sA", "dtA"], w=["thf"])
            p.op("dve", lambda e: e.tensor_copy(out=ki[:], in_=thf[:]), r=["thf"], w=["ki128"])
            p.op("dve", lambda e: e.tensor_tensor(out=thf[:], in0=thf[:], in1=ki[:], op=ALU.subtract), r=["thf", "ki128"], w=["thf"])
            p.op("act", lambda e: e.activation(out=rdec[:], in_=lnr[:], func=AF.Exp), r=["lnr"], w=["rdec"])
            p.op("dve", lambda e: e.tensor_tensor(out=gd[:], in0=self.par[:, PC_NM + 16 * l:PC_NM + 16 * l + 16],
                                                  in1=self.par[:, PC_SD:PC_SD + 16], op=ALU.mult), r=["par"], w=["gd"])
            for i in range(8):
                p.op("dve", lambda e, i=i: e.memset(Lz1[i][:], 0.0), w=[f"Lz1_{i}"])
                p.op("dve", lambda e, i=i: e.memset(Lz2[i][:], 0.0), w=[f"Lz2_{i}"])
            p.op("dve", lambda e: e.memset(Sin0[:], 0.0), w=["Sin0"])
            st2 = contextlib.ExitStack()
            with st2:
                sC = self.sb([128, 2, 128 * 16], F32, "sC", st2)
                p.dma("sp", lambda e: e.dma_start(out=sC[:].rearrange("p a x -> p (a x)"), in_=sC_d[:, :]), "sC", w=["sC"])
                p.op("dve", lambda e: e.tensor_scalar(out=L1[:].rearrange("p g c -> p (g c)"), in0=sC[:, 0, :],
                                                      scalar1=self.par[:, PC_SG1:PC_SG1 + 1], scalar2=None, op0=ALU.mult),
                     r=["sC", "par"], w=["L1"])
                p.op("dve", lambda e: e.tensor_scalar(out=L2[:].rearrange("p g c -> p (g c)"), in0=sC[:, 1, :],
                                                      scalar1=-1.0, scalar2=None, op0=ALU.mult),
                     r=["sC"], w=["L2"])
                p.barrier()
            def bside(hb):
              st3 = contextlib.ExitStack()
              with st3:
                sB = self.sb([128, 5, 512], F32, "sB", st3)
                sBv = sB_d.rearrange("p (a x) -> p a x", a=5)
                p.dma("sp", lambda e, hb=hb: e.dma_start(out=sB[:], in_=sBv[:, :, hb * 512:(hb + 1) * 512]), "sB", w=["sB"])
                tt = [self.sb([128, 512], F32, f"bt{i}", st3) for i in range(8)]
                kb = self.sb([128, 512], I32, "kb", st3)
                Br, Bi, aR, aI, ld = (sB[:, i, :] for i in range(5))
                dtB, er, fr, sn, cs, t5, t6, t7 = (t[:] for t in tt)
                R = ["sB"] + [f"bt{i}" for i in range(8)] + ["kb"]

                def dv(fn):
                    p.op("dve", fn, r=R, w=R)

                def ac(fn):
                    p.op("act", fn, r=R, w=R)
                ac(lambda e: e.activation(out=dtB, in_=ld, func=AF.Exp))
                dv(lambda e: e.tensor_tensor(out=er, in0=aR, in1=dtB, op=ALU.mult))
                ac(lambda e: e.activation(out=er, in_=er, func=AF.Exp))
                dv(lambda e: e.scalar_tensor_tensor(out=fr, in0=aI, scalar=1.0 / TWO_PI, op0=ALU.mult, in1=dtB, op1=ALU.mult))
                dv(lambda e: e.tensor_copy(out=kb[:], in_=fr))
                dv(lambda e: e.tensor_tensor(out=fr, in0=fr, in1=kb[:], op=ALU.subtract))
                ac(lambda e: e.activation(out=sn, in_=fr, func=AF.Sin, scale=TWO_PI))
                ac(lambda e: e.activation(out=cs, in_=fr, func=AF.Sin, scale=math.pi))
                ac(lambda e: e.activation(out=cs, in_=cs, func=AF.Square))
                dv(lambda e: e.tensor_scalar(out=cs, in0=cs, scalar1=-2.0, scalar2=1.0, op0=ALU.mult, op1=ALU.add))
                dv(lambda e: e.tensor_tensor(out=cs, in0=cs, in1=er, op=ALU.mult))
                dv(lambda e: e.tensor_tensor(out=sn, in0=sn, in1=er, op=ALU.mult))
                dv(lambda e: e.tensor_scalar(out=cs, in0=cs, scalar1=-1.0, scalar2=None, op0=ALU.add))
                dv(lambda e: e.tensor_tensor(out=t5, in0=aR, in1=aR, op=ALU.mult))
                dv(lambda e: e.tensor_tensor(out=t6, in0=aI, in1=aI, op=ALU.mult))
                dv(lambda e: e.tensor_tensor(out=t5, in0=t5, in1=t6, op=ALU.add))
                dv(lambda e: e.reciprocal(out=t5, in_=t5))
                dv(lambda e: e.tensor_tensor(out=t6, in0=cs, in1=aR, op=ALU.mult))
                dv(lambda e: e.tensor_tensor(out=t7, in0=sn, in1=aI, op=ALU.mult))
                dv(lambda e: e.tensor_tensor(out=t6, in0=t6, in1=t7, op=ALU.add))
                dv(lambda e: e.tensor_tensor(out=t6, in0=t6, in1=t5, op=ALU.mult))
                dv(lambda e: e.tensor_tensor(out=t7, in0=sn, in1=aR, op=ALU.mult))
                dv(lambda e: e.tensor_tensor(out=fr, in0=cs, in1=aI, op=ALU.mult))
                dv(lambda e: e.tensor_tensor(out=t7, in0=t7, in1=fr, op=ALU.subtract))
                dv(lambda e: e.tensor_tensor(out=t7, in0=t7, in1=t5, op=ALU.mult))
                dv(lambda e: e.tensor_tensor(out=er, in0=t6, in1=Br, op=ALU.mult))
                dv(lambda e: e.tensor_tensor(out=fr, in0=t7, in1=Bi, op=ALU.mult))
                dv(lambda e: e.tensor_tensor(out=er, in0=er, in1=fr, op=ALU.subtract))
                dv(lambda e: e.tensor_tensor(out=sn, in0=t6, in1=Bi, op=ALU.mult))
                dv(lambda e: e.tensor_tensor(out=fr, in0=t7, in1=Br, op=ALU.mult))
                dv(lambda e: e.tensor_tensor(out=sn, in0=sn, in1=fr, op=ALU.add))
                er3 = tt[1][:].rearrange("p (c n) -> p c n", n=64)
                sn3 = tt[3][:].rearrange("p (c n) -> p c n", n=64)
                p.op("dve", lambda e: e.tensor_copy(out=bb1[:, hb * 8:(hb + 1) * 8, 0:64], in_=er3), r=R, w=["bb1"])
                p.op("dve", lambda e: e.tensor_copy(out=bb1[:, hb * 8:(hb + 1) * 8, 64:128], in_=sn3), r=R, w=["bb1"])
                p.op("dve", lambda e: e.tensor_copy(out=bb2[:, hb * 8:(hb + 1) * 8, 0:64], in_=sn3), r=R, w=["bb2"])
                p.op("dve", lambda e: e.tensor_scalar(out=bb2[:, hb * 8:(hb + 1) * 8, 64:128], in0=er3, scalar1=-1.0, scalar2=None, op0=ALU.mult), r=R, w=["bb2"])
                p.barrier()
            bside(0)
            bside(1)
            def horner():
                st4 = contextlib.ExitStack()
                with st4:
                    pm_d = self.dram("perm", [128, 128], F32, "ExternalInput")
                    zall = self.hz_t2[:].rearrange("p (a g) -> p a g", a=8)
                    pm = self.hz_t1[:, 0:128]
                    if mode == "fused":
                        cm = self.cpar[:, 8:16]
                        for k in range(2):
                            self.allgather_words_one(ZE[:], 64 * k, 64, zall[:, :, 64 * k:64 * k + 64], "ze", k, "zall", sres="ZE")
                    else:
                        zall_d = self.dram("zall", [128, 8 * 128], F32, "ExternalInput")
                        cm_d = self.dram("cmask", [128, 8], F32, "ExternalInput")
                        cmt = self.hz_t1[:, 128:136]
                        cm = cmt
                        p.dma("sp", lambda e: e.dma_start(out=self.hz_t2[:], in_=zall_d[:, :]), "zall", w=["zall"])
                        p.dma("sp", lambda e: e.dma_start(out=cmt, in_=cm_d[:, :]), "cm", w=["cm"])
                    p.dma("sp", lambda e: e.dma_start(out=pm, in_=pm_d[:, :]), "pm", w=["pm"])
                    kh = self.hz_kt[:, 0:128]
                    RHO, PF, CF, SFs, X, U = (self.hz_zt[:, 128 * i:128 * (i + 1)] for i in range(6))
                    R = ["zall", "cm", "pm", "kh", "thf", "lnr", "Sin0", "par", "cpar"] + [f"hh{i}" for i in range(6)]

                    def dv(fn, extra_r=()):
                        p.op("dve", fn, r=R + list(extra_r), w=R)

                    def ac(fn):
                        p.op("act", fn, r=R, w=R)
                    ac(lambda e: e.activation(out=RHO, in_=lnr[:], func=AF.Exp, scale=float(T)))
                    dv(lambda e: e.tensor_scalar(out=PF, in0=thf[:], scalar1=float(T), scalar2=None, op0=ALU.mult))
                    dv(lambda e: e.tensor_copy(out=kh, in_=PF))
                    dv(lambda e: e.tensor_tensor(out=PF, in0=PF, in1=kh, op=ALU.subtract))
                    ac(lambda e: e.activation(out=SFs, in_=PF, func=AF.Sin, scale=TWO_PI))
                    dv(lambda e: e.tensor_scalar(out=SFs, in0=SFs, scalar1=self.par[:, PC_SG1:PC_SG1 + 1], scalar2=-1.0,
                                                 op0=ALU.mult, op1=ALU.mult))
                    ac(lambda e: e.activation(out=CF, in_=PF, func=AF.Sin, scale=math.pi))
                    ac(lambda e: e.activation(out=CF, in_=CF, func=AF.Square))
                    dv(lambda e: e.tensor_scalar(out=CF, in0=CF, scalar1=-2.0, scalar2=1.0, op0=ALU.mult, op1=ALU.add))
                    bsw = PS[0][:, 0:128]
                    for j in range(7):
                        dv(lambda e: e.tensor_tensor(out=X, in0=Sin0[:], in1=RHO, op=ALU.mult))
                        dv(lambda e, j=j: e.tensor_tensor(out=X, in0=X, in1=zall[:, j, :], op=ALU.add))
                        p.op("pe", lambda e: e.matmul(bsw, lhsT=pm, rhs=X, start=True, stop=True), r=R, w=["PS0"])
                        dv(lambda e: e.tensor_tensor(out=U, in0=bsw, in1=SFs, op=ALU.mult), extra_r=["PS0"])
                        dv(lambda e: e.tensor_tensor(out=X, in0=X, in1=CF, op=ALU.mult))
                        dv(lambda e: e.tensor_tensor(out=X, in0=X, in1=U, op=ALU.add))
                        dv(lambda e: e.tensor_tensor(out=X, in0=X, in1=Sin0[:], op=ALU.subtract))
                        dv(lambda e, j=j: e.scalar_tensor_tensor(out=Sin0[:], in0=X, scalar=cm[:, j:j + 1], op0=ALU.mult,
                                                                 in1=Sin0[:], op1=ALU.add))
                    p.barrier()
            iota1 = self.sb([128, T], F32, "iota1", st)
            ii = self.sb([128, T], I32, "iotai", st)
            p.op("pool", lambda e: e.iota(ii[:], pattern=[[1, T]], base=1, channel_multiplier=0), w=["iotai"])
            p.op("dve", lambda e: e.tensor_copy(out=iota1[:], in_=ii[:]), r=["iotai"], w=["iota1"])
            kt = ii
            SINs = [self.sb([128, T], F32, f"SINt{i}", st) for i in range(2)]
            COSs = [self.sb([128, T], F32, f"COSt{i}", st) for i in range(2)]
            t1 = self.sb([128, T], F32, "t1", st)
            t2 = self.sb([128, T], F32, "t2", st)
            zt = self.sb([128, T], F32, "zt", st)
            zc = self.sb([128, T], BF16, "zc", st)
            zs = self.sb([128, T], BF16, "zs", st)
            self.hz_t1, self.hz_t2, self.hz_zt, self.hz_kt = t1, t2, zt, kt

            def main_loop(phase_b):
              for ch in range(NCH):
                for gi in range(8):
                    g = 8 * ch + gi
                    b2 = g % 2
                    SINt, COSt = SINs[b2], COSs[b2]
                    sn_, cn_ = f"SINt{b2}", f"COSt{b2}"
                    p.op("dve", lambda e, b2=b2, ch=ch, gi=gi: e.tensor_scalar(
                        out=Bg[2 * b2][:], in0=bb1[:, ch, :], scalar1=self.par[:, PC_GM + gi:PC_GM + gi + 1], scalar2=None, op0=ALU.mult),
                        r=["bb1", "par"], w=[f"Bg{2 * b2}"])
                    p.op("dve", lambda e, b2=b2, ch=ch, gi=gi: e.tensor_scalar(
                        out=Bg[2 * b2 + 1][:], in0=bb2[:, ch, :], scalar1=self.par[:, PC_GM + gi:PC_GM + gi + 1], scalar2=None, op0=ALU.mult),
                        r=["bb2", "par"], w=[f"Bg{2 * b2 + 1}"])
                    for q in range(2):
                        for hf in range(2):
                            p.op("pe", lambda e, q=q, hf=hf, b2=b2, ch=ch: e.matmul(
                                PS[q][:, hf * 512:(hf + 1) * 512], lhsT=Bg[2 * b2 + q][:], rhs=hT[:, ch, HO + hf * 512:HO + (hf + 1) * 512],
                                start=True, stop=True),
                                r=[f"Bg{2 * b2 + q}", f"h{ch}"], w=[f"PS{q}"])
                    thg = thf[:, g:g + 1]
                    p.op("dve", lambda e, thg=thg: e.tensor_scalar(out=kt[:], in0=iota1[:], scalar1=thg, scalar2=None, op0=ALU.mult),
                         r=["iota1", "thf"], w=["kt"])
                    p.op("dve", lambda e, thg=thg, SINt=SINt: e.scalar_tensor_tensor(out=SINt[:], in0=iota1[:], scalar=thg, op0=ALU.mult,
                                                                        in1=kt[:], op1=ALU.subtract),
                         r=["iota1", "thf", "kt"], w=[sn_])
                    p.op("act", lambda e, SINt=SINt, COSt=COSt: e.activation(out=COSt[:], in_=SINt[:], func=AF.Sin, scale=math.pi), r=[sn_], w=[cn_])
                    p.op("act", lambda e, SINt=SINt: e.activation(out=SINt[:], in_=SINt[:], func=AF.Sin, scale=TWO_PI), r=[sn_], w=[sn_])
                    p.op("act", lambda e, COSt=COSt: e.activation(out=COSt[:], in_=COSt[:], func=AF.Square), r=[cn_], w=[cn_])
                    p.op("act", lambda e, COSt=COSt: e.activation(out=COSt[:], in_=COSt[:], func=AF.Identity, scale=-2.0, bias=self.oneb[:]),
                         r=[cn_, "oneb"], w=[cn_])
                    p.op("dve", lambda e, COSt=COSt: e.tensor_tensor(out=t1[:], in0=PS[0][:], in1=COSt[:], op=ALU.mult), r=["PS0", cn_], w=["t1"])
                    p.op("dve", lambda e, SINt=SINt: e.tensor_tensor(out=t2[:], in0=PS[1][:], in1=SINt[:], op=ALU.mult), r=["PS1", sn_], w=["t2"])
                    p.op("pool", lambda e: e.tensor_tensor(out=t1[:], in0=t1[:], in1=t2[:], op=ALU.add), r=["t1", "t2"], w=["t1"])
                    init = Sin0[:, g:g + 1] if phase_b else 0.0
                    p.op("dve", lambda e, g=g, init=init: e.tensor_tensor_scan(
                        out=zt[:], data0=rdec[:, g:g + 1].to_broadcast([128, T]), data1=t1[:], initial=init,
                        op0=ALU.mult, op1=ALU.add),
                        r=["rdec", "t1", "Sin0"], w=["zt"])
                    if not phase_b:
                        p.op("act", lambda e, g=g: e.activation(out=ZE[:, g:g + 1], in_=zt[:, T - 1:T], func=AF.Copy), r=["zt"], w=["ZE"])
                        continue
                    p.op("pool", lambda e, COSt=COSt: e.tensor_tensor(out=zc[:], in0=zt[:], in1=COSt[:], op=ALU.mult), r=["zt", cn_], w=["zc"])
                    p.op("pool", lambda e, SINt=SINt: e.tensor_tensor(out=zs[:], in0=zt[:], in1=SINt[:], op=ALU.mult), r=["zt", sn_], w=["zs"])
                    p.op("dve", lambda e, g=g, gi=gi: e.tensor_copy(out=Lz1[gi][:, 16 * gi:16 * gi + 16], in_=L1[:, g, :]), r=["L1"], w=[f"Lz1_{gi}"])
                    p.op("dve", lambda e, g=g, gi=gi: e.tensor_copy(out=Lz2[gi][:, 16 * gi:16 * gi + 16], in_=L2[:, g, :]), r=["L2"], w=[f"Lz2_{gi}"])
                    for hf in range(2):
                        p.op("pe", lambda e, gi=gi, hf=hf: e.matmul(PS[2][:, hf * 512:(hf + 1) * 512], lhsT=Lz1[gi][:],
                                                                    rhs=zc[:, hf * 512:(hf + 1) * 512], start=(gi == 0), stop=False),
                             r=[f"Lz1_{gi}", "zc"], w=["PS2"])
                        p.op("pe", lambda e, gi=gi, hf=hf: e.matmul(PS[2][:, hf * 512:(hf + 1) * 512], lhsT=Lz2[gi][:],
                                                                    rhs=zs[:, hf * 512:(hf + 1) * 512], start=False, stop=(gi == 7)),
                             r=[f"Lz2_{gi}", "zs"], w=["PS2"])
                if not phase_b:
                    continue
                p.op("dve", lambda e, ch=ch: e.scalar_tensor_tensor(out=t2[:], in0=xT[:, ch, :], scalar=gd[:, ch:ch + 1], op0=ALU.mult,
                                                                   in1=self.rstd[:], op1=ALU.mult),
                     r=[f"x{ch}", "gd", "rstd"], w=["t2"])
                p.op("dve", lambda e: e.tensor_tensor(out=t2[:], in0=PS[2][:], in1=t2[:], op=ALU.add), r=["PS2", "t2"], w=["t2"])
                p.op("act", lambda e: e.activation(out=zt[:], in_=t2[:], func=AF.Square), r=["t2"], w=["zt"])
                p.op("dve", lambda e: e.tensor_scalar(out=zt[:], in0=zt[:], scalar1=0.044715, scalar2=1.0, op0=ALU.mult, op1=ALU.add),
                     r=["zt"], w=["zt"])
                p.op("dve", lambda e: e.tensor_tensor(out=zt[:], in0=zt[:], in1=t2[:], op=ALU.mult), r=["zt", "t2"], w=["zt"])
                p.op("act", lambda e: e.activation(out=zt[:], in_=zt[:], func=AF.Sigmoid, scale=2.0 * math.sqrt(2.0 / math.pi)), r=["zt"], w=["zt"])
                p.op("dve", lambda e, ch=ch: e.tensor_tensor(out=hT[:, ch, HO:HO + T], in0=zt[:], in1=t2[:], op=ALU.mult),
                     r=["zt", "t2"], w=[f"h{ch}"])
            if mode == "a":
                main_loop(False)
                zend = self.dram("zend", [128, 128], F32, "ExternalOutput")
                p.dma("sp", lambda e: e.dma_start(out=zend[:, :], in_=ZE[:]), "zend", r=["ZE"])
                p.barrier()
                return
            if mode == "fused":
                main_loop(False)
                p.barrier()
            horner()
            main_loop(True)
            p.barrier()
            for f in range(NCH):
                sv_ = self.slab(("glu", 0, f))
                sg_ = self.slab(("glu", 1, f))
                for (s, pi) in ((sv_, 0), (sg_, 1)):
                    for k in range(NCH):
                        for hf in range(2):
                            p.op("pe", lambda e, s=s, pi=pi, k=k, hf=hf, f=f: e.matmul(
                                PS[2 * (f % 2) + pi][:, hf * 512:(hf + 1) * 512], lhsT=s.ap[:, k, :],
                                rhs=hT[:, k, HO + hf * 512:HO + (hf + 1) * 512], start=(k == 0), stop=(k == NCH - 1)),
                                r=[s.res, f"h{k}"], w=[f"PS{2 * (f % 2) + pi}"])
                    self.slab_done(s)
                pv, pg = 2 * (f % 2), 2 * (f % 2) + 1
                p.op("act", lambda e, pg=pg: e.activation(out=t1[:], in_=PS[pg][:], func=AF.Sigmoid), r=[f"PS{pg}"], w=["t1"])
                p.op("dve", lambda e, pv=pv: e.tensor_tensor(out=t1[:], in0=PS[pv][:], in1=t1[:], op=ALU.mult), r=[f"PS{pv}", "t1"], w=["t1"])
                p.op("dve", lambda e, f=f: e.tensor_tensor(out=xT[:, f, :], in0=xT[:, f, :], in1=t1[:], op=ALU.add),
                     r=["t1", f"x{f}"], w=[f"x{f}"])
            p.barrier()

    def build(self):
        self.HO = self.H
        self.setup()
        self.ring_init()
        self.xT = self.sb([128, NCH, T], F32, "xT")
        self.hT = self.sb([128, NCH, self.HO + T], BF16, "hT")
        with self.st:
            if self.kind == 0:
                self.load_x()
                self.mixer_conv()
            elif self.kind == 1:
                self.load_x()
                self.mixer_pool()
            elif self.kind == 2:
                self.load_x()
                self.mixer_att()
            elif self.kind == 3:
                self.load_x()
                self.mixer_ssm("a" if self.stage == "ssm_a" else "b")
                if self.stage == "ssm_a":
                    self.p.emit()
                    return self.nc
            if self.stage != "mixer":
                self.mlp()
            self.store_x()
            assert len(self.specs) == self.ns_total or self.stage == "mixer", (len(self.specs), self.ns_total)
            self.p.emit()
        return self.nc


    def build_fused(self):
        self.setup()
        self.cpar_d = self.dram("cpar", [128, 24], F32, "ExternalInput")
        self.cpar = self.sb([128, 24], F32, "cpar")
        self.p.dma("sp", lambda e: e.dma_start(out=self.cpar[:], in_=self.cpar_d[:, :]), "cpar", w=["cpar"])
        self.ring_init()
        self.xT = self.sb([128, NCH, T], F32, "xT")
        with self.st:
            self.load_x()
            for l in self.fused_layers:
                self.layer = l
                self.kind = l % 4
                self.H = HALO[self.kind]
                self.HO = self.H
                lst = contextlib.ExitStack()
                with lst:
                    self.hT = self.sb([128, NCH, self.HO + T], BF16, "hT", lst)
                    if self.kind == 0:
                        self.mixer_conv()
                    elif self.kind == 1:
                        self.mixer_pool()
                    elif self.kind == 2:
                        self.mixer_att()
                    else:
                        self.mixer_ssm("fused")
                    self.mlp()
            self.store_x()
            assert len(self.specs) == self.ns_total, (len(self.specs), self.ns_total)
            self.p.emit()
        return self.nc


def slab_from(w, col0):
    blk = w[:, col0:col0 + 128].reshape(16, 128, 128)
    return np.ascontiguousarray(blk.transpose(1, 0, 2)).reshape(128, 2048)


def make_slabs(specs, inp):
    out = np.empty((len(specs), 128, 2048), np.float32)
    for i, sp in enumerate(specs):
        nm = sp[0]
        if nm == "w1":
            _, l, j = sp
            out[i] = slab_from(inp["mlp_w1"][l], j * 128)
        elif nm == "w2":
            _, l, j = sp
            out[i] = inp["mlp_w2"][l][j * 128:(j + 1) * 128, :]
        elif nm == "conv_in":
            _, part, f = sp
            out[i] = slab_from(inp["conv_w_in"][0], part * D + f * 128)
        elif nm == "conv_out":
            out[i] = slab_from(inp["conv_w_out"][0], sp[1] * 128)
        elif nm == "att_qkv":
            out[i] = slab_from(inp["att_w_qkv"][0], sp[1] * D + sp[2] * 128)
        elif nm == "att_o":
            _, rd, F = sp
            wo = inp["att_w_out"][0]
            blk = wo[rd * 1024:(rd + 1) * 1024, F * 256:(F + 1) * 256].reshape(8, 128, 2, 128)
            out[i] = np.ascontiguousarray(blk.transpose(1, 2, 0, 3)).reshape(128, 2048)
        elif nm == "glu":
            out[i] = slab_from(inp["ssm_w_glu"][0], sp[1] * D + sp[2] * 128)
        elif nm == "pool_in":
            out[i] = slab_from(inp["pool_w_in"][0], sp[1] * 128)
        elif nm == "pool_g":
            wg = inp["pool_w_group"][0][sp[1]]
            blk = wg.reshape(4, 128, 4, 128)
            out[i] = np.ascontiguousarray(blk.transpose(1, 2, 0, 3)).reshape(128, 2048)
        else:
            raise KeyError(sp)
    return out


def colmajor16(v):
    return np.ascontiguousarray(v.reshape(16, 128).T)


def make_par(inp):
    par = np.zeros((128, NPAR), np.float32)
    for l in range(4):
        par[:, PC_NM + 16 * l:PC_NM + 16 * l + 16] = colmajor16(inp["norm_mix"][l])
        par[:, PC_NF + 16 * l:PC_NF + 16 * l + 16] = colmajor16(inp["norm_mlp"][l])
    for k in range(3):
        par[:, PC_CW + 16 * k:PC_CW + 16 * k + 16] = colmajor16(inp["conv_w"][0][k])
    par[:, PC_PS:PC_PS + 16] = colmajor16(inp["pool_scale"][0])
    par[:, PC_QG] = inp["att_q_norm"][0]
    par[:, PC_KG] = inp["att_k_norm"][0]
    par[:, PC_SD:PC_SD + 16] = colmajor16(inp["ssm_d"][0])
    par[:64, PC_SG1] = 1.0
    par[64:, PC_SG1] = -1.0
    for gi in range(8):
        par[16 * gi:16 * gi + 16, PC_GM + gi] = 1.0
    return par


def make_bias_table(rel_bias):
    kk = np.arange(128)[:, None, None]
    j = np.arange(5)[None, :, None]
    qq = np.arange(128)[None, None, :]
    delta = 512 - 128 * j + qq - kk
    rel = np.clip(delta, -256, 256) + 256
    dc = qq // 64 - 2 * j - kk // 64 + 8
    vis = (dc >= 0) & (dc <= 8)
    tab = rel_bias[:, rel]
    tab = np.where(vis[None], tab, np.float32(-1e30)).astype(np.float32)
    return np.ascontiguousarray(tab.reshape(16, 128, 640))


def make_ssm_inputs(inp):
    f32 = np.float32
    a_re = inp["ssm_a_re"][0]; a_im = inp["ssm_a_im"][0]; ldt = inp["ssm_log_dt"][0]
    b_re = inp["ssm_b_re"][0]; b_im = inp["ssm_b_im"][0]
    c_re = inp["ssm_c_re"][0]; c_im = inp["ssm_c_im"][0]
    sA = np.empty((128, 3, 128), f32)
    sA[:, 0, :] = np.tile(a_re.T, (2, 1))
    sA[:, 1, :] = np.tile(a_im.T, (2, 1))
    sA[:, 2, :] = ldt[None, :]

    def layB(b):
        return b.reshape(16, 8, 64, 16).transpose(1, 3, 0, 2).reshape(128, 16, 64)

    def layA(a):
        t = a.reshape(16, 8, 64).transpose(1, 0, 2)
        return np.broadcast_to(t[:, None], (8, 16, 16, 64)).reshape(128, 16, 64)
    ldtB = np.broadcast_to(ldt.reshape(16, 8).T[:, None, :, None], (8, 16, 16, 64)).reshape(128, 16, 64)
    sB = np.stack([layB(b_re), layB(b_im), layA(a_re), layA(a_im), ldtB], axis=1).reshape(128, 5 * 1024)
    cr = c_re.transpose(2, 0, 1)
    ci = c_im.transpose(2, 0, 1)
    sC = np.empty((128, 2, 128, 16), f32)
    sC[:64, 0] = cr; sC[64:, 0] = ci; sC[:64, 1] = ci; sC[64:, 1] = cr
    perm = np.zeros((128, 128), f32)
    perm[np.arange(128), (np.arange(128) + 64) % 128] = 1.0
    return dict(sA=np.ascontiguousarray(sA.reshape(128, 384)), sB=np.ascontiguousarray(sB.astype(f32)),
                sC=np.ascontiguousarray(sC.reshape(128, 4096))), perm


_CACHE = {}
_TRACE = [False]
_TIMES = []


def get_prog(layer, stage="full"):
    key = (layer, stage)
    if key not in _CACHE:
        b = Builder(layer, stage)
        nc = b.build()
        _CACHE[key] = (nc, b.specs)
    return _CACHE[key]


def run_layer(layer, xfull, inp, stage="full", cores=None):
    H = HALO[layer % 4]
    par = make_par(inp)
    cores = list(range(NCORE)) if cores is None else cores
    extra = {}
    if layer % 4 == 3:
        ssm_in, perm = make_ssm_inputs(inp)
        nca, _ = get_prog(layer, "ssm_a")
        maps_a = [dict(par=par, xin=np.ascontiguousarray(xfull[c * T:(c + 1) * T].T), **ssm_in) for c in cores]
        ra = run_bass_kernel_spmd(nca, maps_a, core_ids=list(range(len(cores))), **({"trace": True} if _TRACE[0] else {}))
        _TIMES.append(("ssm_a", ra.exec_time_ns))
        zall = np.zeros((128, 8, 128), np.float32)
        for i, c in enumerate(cores):
            zall[:, c, :] = ra.results[i]["zend"]
        extra = dict(ssm_in, perm=perm, zall=np.ascontiguousarray(zall.reshape(128, 1024)))
    nc, specs = get_prog(layer, stage)
    wsl = make_slabs(specs, inp)
    in_maps = []
    for c in cores:
        m = {"wsl": wsl, "par": par,
             "xin": np.ascontiguousarray(xfull[c * T:(c + 1) * T].T)}
        m.update(extra)
        if layer % 4 == 3:
            cmk = np.zeros((128, 8), np.float32)
            cmk[:, :c] = 1.0
            m["cmask"] = cmk
        if layer % 4 == 1:
            pos = np.arange(c * T + 1, (c + 1) * T + 1, dtype=np.float32)
            ic = np.stack([1.0 / np.minimum(pos, float(w)) for w in (2, 4, 8, 16)]).astype(np.float32)
            m["invc"] = np.ascontiguousarray(np.broadcast_to(ic.reshape(1, 4 * T), (128, 4 * T)))
        if layer % 4 == 2:
            m["biasT"] = make_bias_table(inp["att_rel_bias"][0])
            m["hmask"] = np.full((128, 128), 0.0 if c == 0 else 1.0, np.float32)
        if H:
            xh = np.zeros((D, H), np.float32)
            if c > 0:
                xh[:] = xfull[c * T - H:c * T].T
            m["xh"] = xh
        in_maps.append(m)
    res = run_bass_kernel_spmd(nc, in_maps, core_ids=list(range(len(cores))), **({"trace": True} if _TRACE[0] else {}))
    _TIMES.append((layer, res.exec_time_ns))
    out = np.empty((len(cores) * T, D), np.float32)
    for i, r in enumerate(res.results):
        out[i * T:(i + 1) * T] = r["xout"].T
    return out


def get_fused(layers):
    key = ("fused", tuple(layers))
    if key not in _CACHE:
        b = Builder(layers[0], "full", fused_layers=list(layers))
        nc = b.build_fused()
        _CACHE[key] = (nc, b.specs)
    return _CACHE[key]


def run_fused(xfull, inp, layers=(0, 1, 2, 3), cores=None):
    nc, specs = get_fused(layers)
    cores = list(range(NCORE)) if cores is None else cores
    wsl = make_slabs(specs, inp)
    par = make_par(inp)
    common = {"wsl": wsl, "par": par}
    if 2 in layers:
        common["biasT"] = make_bias_table(inp["att_rel_bias"][0])
    if 3 in layers:
        ssm_in, perm = make_ssm_inputs(inp)
        common.update(ssm_in)
        common["perm"] = perm
    in_maps = []
    for c in cores:
        m = dict(common)
        m["xin"] = np.ascontiguousarray(xfull[c * T:(c + 1) * T].T)
        cp = np.zeros((128, 24), np.float32)
        if c > 0:
            cp[:, c - 1] = 1.0
            cp[:, 16] = 1.0
        cp[:, 8:8 + c] = 1.0
        m["cpar"] = cp
        if 1 in layers:
            pos = np.arange(c * T + 1, (c + 1) * T + 1, dtype=np.float32)
            ic = np.stack([1.0 / np.minimum(pos, float(w)) for w in (2, 4, 8, 16)]).astype(np.float32)
            m["invc"] = np.ascontiguousarray(np.broadcast_to(ic.reshape(1, 4 * T), (128, 4 * T)))
        in_maps.append(m)
    res = run_bass_kernel_spmd(nc, in_maps, core_ids=list(range(len(cores))))
    out = np.empty((len(cores) * T, D), np.float32)
    for i, r in enumerate(res.results):
        out[i * T:(i + 1) * T] = r["xout"].T
    return out


def kernel(**inputs):
    inp = {k: np.asarray(v) for k, v in inputs.items()}
    x = np.ascontiguousarray(inp["x"][0])
    for layer in range(4):
        x = run_layer(layer, x, inp)
    return x[None].astype(np.float32)
```
